# Optimizing a Trainium2 kernel written in Bass

```python
import jax, jax.numpy as jnp
from jax import lax
import numpy as np

D_MODEL = 4096
BATCH = 1
SEQ = 8192
DEPTH = 2
DEC_BATCH = 16
DEC_SEQ = 64
PAST_LEN = 4096

CHUNK = 64
CONV_DIM = 1024
CONV_WIDTH = 3
ATT_HEADS = 8
ATT_HEAD_DIM = 128
ATT_DIM = ATT_HEADS * ATT_HEAD_DIM
BAND_CHUNKS = 8
WINDOW = BAND_CHUNKS * CHUNK
REL_CLIP = 128
GLA_HEADS = 4
GLA_DK = 256
GLA_DV = 512
GLA_KDIM = GLA_HEADS * GLA_DK
GLA_VDIM = GLA_HEADS * GLA_DV
GLA_RANK = 16
GLA_TAU = 16.0
GLA_BLOCK = 16
MIX_DIM = CONV_DIM + ATT_DIM + GLA_VDIM
D_FF = ((8 * D_MODEL // 3 + 255) // 256) * 256
NEG_INF = -1e30
IN_SPLITS = (CONV_DIM, CONV_DIM, CONV_DIM,
             ATT_DIM, ATT_DIM, ATT_DIM,
             GLA_KDIM, GLA_KDIM, GLA_VDIM, GLA_VDIM, GLA_RANK,
             D_MODEL, D_MODEL, D_MODEL)
IN_DIM = sum(IN_SPLITS)

kernel_name = 'hybrid_stream_conv_bandattn_gla_step'


def _split_points():
    return [int(v) for v in np.cumsum(IN_SPLITS)[:-1]]


def rmsnorm(x, g, eps=1e-6):
    xf = x.astype(jnp.float32)
    y = xf * lax.rsqrt(jnp.mean(xf * xf, axis=-1, keepdims=True) + eps)
    return (y * g.astype(jnp.float32)).astype(x.dtype)


def short_conv(b_gate, c_gate, u_in, buf, w):
    u = c_gate * u_in
    t = u.shape[1]
    full = jnp.concatenate([buf.astype(u.dtype), u], axis=1)
    y = full[:, 0:t] * w[0]
    for j in range(1, CONV_WIDTH):
        y = y + full[:, j:j + t] * w[j]
    return b_gate * y, full[:, t:]


def band_attention_prompt(q, k, v, rel_bias):
    n, t, h, dh = q.shape
    nc = t // CHUNK
    span = (BAND_CHUNKS + 1) * CHUNK
    qc = q.reshape(n, nc, CHUNK, h, dh)
    pad = ((0, 0), (WINDOW, 0), (0, 0), (0, 0))
    band = jnp.arange(nc)[:, None] + jnp.arange(BAND_CHUNKS + 1)[None, :]
    kb = jnp.pad(k, pad).reshape(n, nc + BAND_CHUNKS, CHUNK, h, dh)[:, band].reshape(n, nc, span, h, dh)
    vb = jnp.pad(v, pad).reshape(n, nc + BAND_CHUNKS, CHUNK, h, dh)[:, band].reshape(n, nc, span, h, dh)
    s = jnp.einsum('ncqhd,nckhd->nchqk', qc, kb, preferred_element_type=jnp.float32) * (dh ** -0.5)
    j = jnp.arange(span)
    rel = jnp.clip(jnp.arange(CHUNK)[:, None] - j[None, :] + WINDOW, -REL_CLIP, REL_CLIP) + REL_CLIP
    s = s + rel_bias.astype(jnp.float32)[:, rel][None, None]
    kpos = jnp.arange(nc)[:, None] * CHUNK - WINDOW + j[None, :]
    s = jnp.where((kpos >= 0)[None, :, None, None, :], s, NEG_INF)
    p = jax.nn.softmax(s, axis=-1).astype(v.dtype)
    o = jnp.einsum('nchqk,nckhd->ncqhd', p, vb)
    return o.reshape(n, t, h * dh)


def band_attention_sample(q, k, v, k_cache, v_cache, rel_bias):
    n, s_len, h, dh = q.shape
    w = k_cache.shape[1]
    kk = jnp.concatenate([k_cache.astype(k.dtype), k], axis=1)
    vv = jnp.concatenate([v_cache.astype(v.dtype), v], axis=1)
    sc = jnp.einsum('nqhd,nkhd->nhqk', q, kk, preferred_element_type=jnp.float32) * (dh ** -0.5)
    rel = jnp.clip((w + jnp.arange(s_len))[:, None] - jnp.arange(w + s_len)[None, :],
                   -REL_CLIP, REL_CLIP) + REL_CLIP
    sc = sc + rel_bias.astype(jnp.float32)[:, rel][None]
    p = jax.nn.softmax(sc, axis=-1).astype(v.dtype)
    o = jnp.einsum('nhqk,nkhd->nqhd', p, vv)
    return o.reshape(n, s_len, h * dh)


def gla_recurrence(q, k, v, log_a, s0):
    n, t, h, dk = q.shape
    dv = v.shape[-1]
    out_dtype = v.dtype
    tp = -(-t // GLA_BLOCK) * GLA_BLOCK
    nb = tp // GLA_BLOCK

    def blocks(a):
        a = jnp.pad(a.astype(jnp.float32), ((0, 0), (0, tp - t), (0, 0), (0, 0)))
        return a.reshape(n, nb, GLA_BLOCK, h, a.shape[-1]).swapaxes(0, 1)

    causal = jnp.tril(jnp.ones((GLA_BLOCK, GLA_BLOCK), dtype=bool))

    def step(S, blk):
        qb, kb, vb, ab = blk
        b = jnp.cumsum(ab, axis=1)
        b_last = b[:, -1]
        q_dec = qb * jnp.exp(b)
        k_inv = kb * jnp.exp(-b)
        k_rem = kb * jnp.exp(b_last[:, None] - b)
        o_inter = jnp.einsum('nlhk,nhkv->nlhv', q_dec, S)
        att = jnp.where(causal, jnp.einsum('nlhk,nmhk->nhlm', q_dec, k_inv), 0.0)
        o_intra = jnp.einsum('nhlm,nmhv->nlhv', att, vb)
        S = jnp.exp(b_last)[..., None] * S + jnp.einsum('nlhk,nlhv->nhkv', k_rem, vb)
        return S, o_inter + o_intra

    S, o = lax.scan(step, s0.astype(jnp.float32), (blocks(q), blocks(k), blocks(v), blocks(log_a)))
    o = o.swapaxes(0, 1).reshape(n, tp, h, dv)[:, :t]
    return o.astype(out_dtype), S.astype(s0.dtype)


def trunk_layer(x, conv_buf, k_cache, v_cache, gla_s0, g_mix, w_in, conv_w, g_q, g_k, rel_bias,
                w_a2, b_a, g_gla, w_branch, w_out, g_ffn, w_gu, w_down, is_prompt):
    n, t, _ = x.shape
    h = rmsnorm(x, g_mix)
    z = h @ w_in
    (c_b, c_c, c_x, a_q, a_k, a_v, l_q, l_k, l_v, l_r, l_a,
     gate_a, gate_b, gate_c) = jnp.split(z, _split_points(), axis=-1)
    y_a, conv_new = short_conv(c_b, c_c, c_x, conv_buf, conv_w)
    q = rmsnorm(a_q.reshape(n, t, ATT_HEADS, ATT_HEAD_DIM), g_q)
    k = rmsnorm(a_k.reshape(n, t, ATT_HEADS, ATT_HEAD_DIM), g_k)
    v = a_v.reshape(n, t, ATT_HEADS, ATT_HEAD_DIM)
    if is_prompt:
        y_b = band_attention_prompt(q, k, v, rel_bias)
        keep = min(WINDOW, t)
        k_new, v_new = k[:, t - keep:], v[:, t - keep:]
    else:
        y_b = band_attention_sample(q, k, v, k_cache, v_cache, rel_bias)
        k_new, v_new = k, v
    log_a = jax.nn.log_sigmoid((l_a @ w_a2 + b_a).astype(jnp.float32)) / GLA_TAU
    o, gla_new = gla_recurrence(
        (l_q * (GLA_DK ** -0.5)).reshape(n, t, GLA_HEADS, GLA_DK),
        l_k.reshape(n, t, GLA_HEADS, GLA_DK),
        l_v.reshape(n, t, GLA_HEADS, GLA_DV),
        log_a.reshape(n, t, GLA_HEADS, GLA_DK),
        gla_s0)
    y_c = rmsnorm(o, g_gla).reshape(n, t, GLA_VDIM) * jax.nn.silu(l_r)
    wb_a, wb_b, wb_c = jnp.split(w_branch, [CONV_DIM, CONV_DIM + ATT_DIM], axis=0)
    m = (jax.nn.sigmoid(gate_a) * (y_a @ wb_a)
         + jax.nn.sigmoid(gate_b) * (y_b @ wb_b)
         + jax.nn.sigmoid(gate_c) * (y_c @ wb_c))
    x = x + m @ w_out
    u_g, u_u = jnp.split(rmsnorm(x, g_ffn) @ w_gu, 2, axis=-1)
    x = x + (jax.nn.silu(u_g) * u_u) @ w_down
    return x, conv_new, k_new, v_new, gla_new


def setup_inputs(seed: int = 0) -> dict:
    key = jax.random.key(seed)
    ks = jax.random.split(key, 24)

    def nrm(k, shape, scale):
        return jax.random.normal(k, shape, jnp.float32) * scale

    def gain(k, shape):
        return 1.0 + 0.01 * jax.random.normal(k, shape, jnp.float32)

    win_rows = min(WINDOW, PAST_LEN)
    return {
        'x_prompt': nrm(ks[0], (BATCH, SEQ, D_MODEL), 1.0),
        'x_sample': nrm(ks[1], (DEC_BATCH, DEC_SEQ, D_MODEL), 1.0),
        'cache_conv': nrm(ks[2], (DEPTH, DEC_BATCH, CONV_WIDTH - 1, CONV_DIM), 1.0),
        'cache_k': nrm(ks[3], (DEPTH, DEC_BATCH, win_rows, ATT_HEADS, ATT_HEAD_DIM), 1.0),
        'cache_v': nrm(ks[4], (DEPTH, DEC_BATCH, win_rows, ATT_HEADS, ATT_HEAD_DIM), 1.0),
        'state_gla': nrm(ks[5], (DEPTH, DEC_BATCH, GLA_HEADS, GLA_DK, GLA_DV), 1.0),
        'g_mix': gain(ks[6], (DEPTH, D_MODEL)),
        'w_in': nrm(ks[7], (DEPTH, D_MODEL, IN_DIM), D_MODEL ** -0.5),
        'conv_w': nrm(ks[8], (DEPTH, CONV_WIDTH, CONV_DIM), CONV_WIDTH ** -0.5),
        'g_q': gain(ks[9], (DEPTH, ATT_HEAD_DIM)),
        'g_k': gain(ks[10], (DEPTH, ATT_HEAD_DIM)),
        'rel_bias': nrm(ks[11], (DEPTH, ATT_HEADS, 2 * REL_CLIP + 1), 0.1),
        'w_a2': nrm(ks[12], (DEPTH, GLA_RANK, GLA_KDIM), GLA_RANK ** -0.5),
        'b_a': nrm(ks[13], (DEPTH, GLA_KDIM), 0.1),
        'g_gla': gain(ks[14], (DEPTH, GLA_DV)),
        'w_branch': nrm(ks[15], (DEPTH, MIX_DIM, D_MODEL), CONV_DIM ** -0.5),
        'w_out': nrm(ks[16], (DEPTH, D_MODEL, D_MODEL), D_MODEL ** -0.5),
        'g_ffn': gain(ks[17], (DEPTH, D_MODEL)),
        'w_gu': nrm(ks[18], (DEPTH, D_MODEL, 2 * D_FF), D_MODEL ** -0.5),
        'w_down': nrm(ks[19], (DEPTH, D_FF, D_MODEL), D_FF ** -0.5),
    }


def reference(x_prompt, x_sample, cache_conv, cache_k, cache_v, state_gla, g_mix, w_in, conv_w,
              g_q, g_k, rel_bias, w_a2, b_a, g_gla, w_branch, w_out, g_ffn, w_gu, w_down):
    xp, xs = x_prompt, x_sample
    nb = xp.shape[0]
    conv_p, k_p, v_p, gla_p = [], [], [], []
    conv_s, k_s, v_s, gla_s = [], [], [], []
    for l in range(DEPTH):
        params = (g_mix[l], w_in[l], conv_w[l], g_q[l], g_k[l], rel_bias[l], w_a2[l], b_a[l],
                  g_gla[l], w_branch[l], w_out[l], g_ffn[l], w_gu[l], w_down[l])
        xp, c_new, k_new, v_new, s_new = trunk_layer(
            xp, jnp.zeros((nb, CONV_WIDTH - 1, CONV_DIM), xp.dtype), None, None,
            jnp.zeros((nb, GLA_HEADS, GLA_DK, GLA_DV), xp.dtype), *params, is_prompt=True)
        conv_p.append(c_new); k_p.append(k_new); v_p.append(v_new); gla_p.append(s_new)
        xs, c_new, k_new, v_new, s_new = trunk_layer(
            xs, cache_conv[l], cache_k[l], cache_v[l], state_gla[l], *params, is_prompt=False)
        conv_s.append(c_new); k_s.append(k_new); v_s.append(v_new); gla_s.append(s_new)
    return (xp, xs,
            jnp.stack(conv_p), jnp.stack(k_p), jnp.stack(v_p), jnp.stack(gla_p),
            jnp.stack(conv_s), jnp.stack(k_s), jnp.stack(v_s), jnp.stack(gla_s))
```

```python
import numpy as np
import concourse.bass as bass
import concourse.mybir as mybir
from concourse.bass_utils import run_bass_kernel_spmd

F32 = mybir.dt.float32
BF16 = mybir.dt.bfloat16
AF = mybir.ActivationFunctionType
ALU = mybir.AluOpType

D = 4096
NSEQ = 2
SEQ_S = 64
TG = 512
IN_DIM = 24592
D_FF = 11008
CONV = 1024
EPS = 1e-6
C_CB, C_CC, C_CX = 0, 1024, 2048
C_AQ, C_AK, C_AV = 3072, 4096, 5120
C_LQ, C_LK, C_LV, C_LR, C_LA = 6144, 7168, 8192, 10240, 12288
C_GA = 12304
ZB_CB, ZB_CC, ZB_CX, ZB_AQ, ZB_AK, ZB_LQ, ZB_LK, ZB_G = 0, 8, 16, 24, 32, 40, 48, 56
NZF = 56 + 96
ZT_AV, ZT_LK, ZT_LV, ZT_LR = 0, 1024, 2048, 4096
NZT = 6144


class G:
    def __init__(self, name):
        self.name = name


class Sched:
    EPOCH = 12000
    NSLOT = 12
    DMAX = 30000

    def __init__(self, nc):
        self.nc = nc
        self.eng = {'pe': nc.tensor, 'act': nc.scalar, 'dve': nc.vector, 'pool': nc.gpsimd, 'sp': nc.sync}
        self.cnt = {e: 0 for e in self.eng}
        self.sems = {e: [] for e in self.eng}
        self.seen = {e: {} for e in self.eng}
        self.res = {}
        self.groups = {}
        self.dslots = {q: [[None, 0, 0, i] for i in range(self.NSLOT)] for q in ('sp', 'pool')}
        self.dnext = {'sp': 0, 'pool': 0}
        self.nsem = 0

    def _newsem(self, name):
        self.nsem += 1
        return self.nc.alloc_semaphore(f"{name}_{self.nsem}")

    def _esem(self, e, c):
        i = (c - 1) // self.EPOCH
        while len(self.sems[e]) <= i:
            self.sems[e].append(self._newsem("e" + e))
        return self.sems[e][i], (c - 1) % self.EPOCH + 1, ('e', e, i)

    def _wait(self, e, tok):
        if tok is None:
            return
        if tok[0] == 'c':
            _, o, c = tok
            sem, val, key = self._esem(o, c)
        else:
            _, sem, val, key = tok
        if self.seen[e].get(key, 0) >= val:
            return
        self.seen[e][key] = val
        self.eng[e].wait_ge(sem, val)

    def _expand(self, keys):
        out = []
        for k in keys:
            if isinstance(k, G):
                out.extend(self.groups.get(k.name, ()))
            else:
                out.append(k)
        return out

    def _deps(self, e, reads, writes):
        for k in reads:
            r = self.res.get(k)
            if r:
                self._wait(e, r['w'])
        for k in writes:
            r = self.res.get(k)
            if r:
                self._wait(e, r['w'])
                for t in r['r']:
                    self._wait(e, t)

    def _commit(self, tok, reads, writes):
        for k in reads:
            r = self.res.setdefault(k, {'w': None, 'r': []})
            r['r'].append(tok)
            if len(r['r']) > 16:
                r['r'] = r['r'][-16:]
        for k in writes:
            self.res[k] = {'w': tok, 'r': []}
            if isinstance(k, tuple):
                self.groups.setdefault(k[0], set()).add(k)

    def op(self, e, reads, writes, emit):
        reads = self._expand(reads)
        writes = self._expand(writes)
        self._deps(e, reads, writes)
        inst = emit()
        self.cnt[e] += 1
        sem, val, _ = self._esem(e, self.cnt[e])
        inst.then_inc(sem, 1)
        self._commit(('c', e, self.cnt[e]), reads, writes)

    def dma(self, q, out, in_, reads, writes, **kw):
        reads = self._expand(reads)
        writes = self._expand(writes)
        self._deps(q, reads, writes)
        i = self.dnext[q]
        self.dnext[q] = (i + 1) % self.NSLOT
        slot = self.dslots[q][i]
        if slot[0] is None or slot[1] >= self.DMAX:
            if slot[0] is not None:
                self.eng[q].wait_ge(slot[0], slot[1])
            slot[0] = self._newsem("d" + q)
            slot[1] = 0
            slot[2] += 1
        elif slot[1] > 0:
            self.eng[q].wait_ge(slot[0], slot[1])
        inst = self.eng[q].dma_start(out=out, in_=in_, **kw)
        slot[1] += 16
        inst.then_inc(slot[0], 16)
        tok = ('d', slot[0], slot[1], ('d', q, slot[3], slot[2]))
        self._commit(tok, reads, writes)

    def barrier(self):
        toks = [('c', e, self.cnt[e]) for e in self.eng if self.cnt[e] > 0]
        for q in ('sp', 'pool'):
            for slot in self.dslots[q]:
                if slot[0] is not None and slot[1] > 0:
                    toks.append(('d', slot[0], slot[1], ('d', q, slot[3], slot[2])))
        for e in self.eng:
            for t in toks:
                if not (t[0] == 'c' and t[1] == e):
                    self._wait(e, t)

    def finish(self):
        self.barrier()


class Mem:
    def __init__(self, nc, lo, hi):
        self.nc, self.lo, self.hi, self.cur, self.n = nc, lo, hi, lo, 0

    def alloc(self, shape, dt, name="t"):
        nbytes = int(np.prod(shape[1:])) * (2 if dt == BF16 else 4)
        off = (self.cur + 31) // 32 * 32
        assert off + nbytes <= self.hi, (name, off, nbytes, self.hi)
        self.cur = off + nbytes
        self.n += 1
        return self.nc.alloc_sbuf_tensor_at(f"{name}{self.n}", list(shape), dt, offset=off)


def build_program(nprompt=8192, depth=2):
    NTOK = nprompt + NSEQ * SEQ_S
    nc = bass.Bass("TRN2", target_bir_lowering=False)
    S = Sched(nc)
    dt_ = nc.dram_tensor
    EI, EO = "ExternalInput", "ExternalOutput"
    NCK = min(512, nprompt)

    x_all = dt_("x_all", [NTOK, D], F32, kind=EI).ap()
    cache_conv = dt_("cache_conv", [depth, NSEQ, 2, CONV], F32, kind=EI).ap()
    cache_k = dt_("cache_k", [depth, NSEQ, 512, 1024], F32, kind=EI).ap()
    cache_v = dt_("cache_v", [depth, NSEQ, 512, 1024], F32, kind=EI).ap()
    state_gla = dt_("state_gla", [depth, NSEQ, 4, 256, 512], F32, kind=EI).ap()
    g_mix = dt_("g_mix", [depth, D], F32, kind=EI).ap()
    w_in = dt_("w_in", [depth, D, IN_DIM], F32, kind=EI).ap()
    conv_w = dt_("conv_w", [depth, 3, CONV], F32, kind=EI).ap()
    g_q = dt_("g_q", [depth, 128], F32, kind=EI).ap()
    g_k = dt_("g_k", [depth, 128], F32, kind=EI).ap()
    rel_bias = dt_("rel_bias", [depth, 8, 257], F32, kind=EI).ap()
    w_a2 = dt_("w_a2", [depth, 16, 1024], F32, kind=EI).ap()
    b_a = dt_("b_a", [depth, 1024], F32, kind=EI).ap()
    g_gla = dt_("g_gla", [depth, 512], F32, kind=EI).ap()
    w_branch = dt_("w_branch", [depth, D, D], F32, kind=EI).ap()
    w_out = dt_("w_out", [depth, D, D], F32, kind=EI).ap()
    g_ffn = dt_("g_ffn", [depth, D], F32, kind=EI).ap()
    w_gu = dt_("w_gu", [depth, D, 2 * D_FF], F32, kind=EI).ap()
    w_down = dt_("w_down", [depth, D_FF, D], F32, kind=EI).ap()
    c_ident = dt_("c_ident", [128, 128], F32, kind=EI).ap()
    c_anti = dt_("c_anti", [64, 64], F32, kind=EI).ap()
    c_tri = dt_("c_tri", [128, 128], F32, kind=EI).ap()
    c_triu = dt_("c_triu", [128, 128], F32, kind=EI).ap()

    y_all = dt_("y_all", [NTOK, D], F32, kind=EO).ap()
    o_conv_p = dt_("o_conv_p", [depth, 2, CONV], F32, kind=EO).ap()
    o_k_p = dt_("o_k_p", [depth, NCK, 1024], F32, kind=EO).ap()
    o_v_p = dt_("o_v_p", [depth, NCK, 1024], F32, kind=EO).ap()
    o_gla_p = dt_("o_gla_p", [depth, 4, 256, 512], F32, kind=EO).ap()
    o_conv_s = dt_("o_conv_s", [depth, NSEQ, 2, CONV], F32, kind=EO).ap()
    o_k_s = dt_("o_k_s", [depth, NSEQ, 64, 1024], F32, kind=EO).ap()
    o_v_s = dt_("o_v_s", [depth, NSEQ, 64, 1024], F32, kind=EO).ap()
    o_gla_s = dt_("o_gla_s", [depth, NSEQ, 4, 256, 512], F32, kind=EO).ap()

    XA = dt_("XA", [NTOK, D], F32).ap()
    XM = dt_("XM", [TG, D], F32).ap()
    ZF = dt_("ZF", [NZF, 128, TG], F32).ap()
    ZT = dt_("ZT", [TG, NZT], F32).ap()
    YT = dt_("YT", [32, 128, TG], BF16).ap()
    seqlen = [nprompt] + [SEQ_S] * NSEQ
    UH = [dt_(f"UH{s}", [8, 128, 2 + seqlen[s]], F32).ap() for s in range(1 + NSEQ)]
    KH = [dt_(f"KH{s}", [8, 128, 512 + seqlen[s]], BF16).ap() for s in range(1 + NSEQ)]
    VH = [dt_(f"VH{s}", [512 + seqlen[s], 1024], BF16).ap() for s in range(1 + NSEQ)]
    REXT = dt_("REXT", [8, 768], F32).ap()
    BTD = dt_("BTD", [8, 64, 576], F32).ap()
    NWEL = D * IN_DIM + 2 * D * D + D * 2 * D_FF + D_FF * D
    WCH = 125 * 1024 * 1024
    WBF = [[dt_(f"WBF{l}_{c}", [WCH], BF16).ap() for c in range((NWEL + WCH - 1) // WCH + 1)] for l in range(depth)]

    M = Mem(nc, 16512, 229344)
    actT = M.alloc([128, 32, TG], BF16, "actT")
    NW = 2
    wbuf = [M.alloc([128, 43 * 256], BF16, "wbuf") for _ in range(NW)]
    ident_f = M.alloc([128, 128], F32, "identf")
    ident_b = M.alloc([128, 128], BF16, "identb")
    ones_f = M.alloc([128, 128], F32, "onesf")
    ones_b = M.alloc([128, 128], BF16, "onesb")
    tri = M.alloc([128, 128], F32, "tri")
    triu = M.alloc([128, 128], F32, "triu")
    anti = M.alloc([64, 64], F32, "anti")
    ev = [M.alloc([128, TG], F32, "ev") for _ in range(4)]
    xr = [M.alloc([128, 256], F32, "xr") for _ in range(4)]
    small = M.alloc([128, 16], F32, "small")
    Sst = M.alloc([128, 8, 512], F32, "Sst")
    Sbf = M.alloc([128, 8, 512], BF16, "Sbf")
    LA = M.alloc([17, TG], F32, "LA")
    WA2B = M.alloc([17, 1024], F32, "WA2B")
    gqk = M.alloc([128, 2], F32, "gqk")
    cw = M.alloc([128, 8, 3], F32, "cw")
    zero_t = M.alloc([128, 16], F32, "zero")
    rext_sb = M.alloc([8, 768], F32, "rext")
    OV0 = M.cur
    ps = nc.alloc_psum_tensor("ps", [128, 8, 512], F32)

    Md = Mem(nc, OV0, 229344)
    gtile = Md.alloc([128, D], F32, "gtile")
    xt = Md.alloc([128, D], F32, "xt")
    hb = Md.alloc([128, D], BF16, "hb")
    aT = Md.alloc([128, 43, TG], BF16, "aT")
    Mb = Mem(nc, OV0, 229344)
    mT = Mb.alloc([128, 32, TG], BF16, "mT")
    macc = Mb.alloc([128, TG], F32, "macc")
    gts = [Mb.alloc([128, TG], F32, "gts") for _ in range(3)]
    Mm = Mem(nc, OV0, 229344)
    m_f = [Mm.alloc([128, TG + 2], F32, "mf") for _ in range(6)]
    m_b = [Mm.alloc([128, TG], BF16, "mb") for _ in range(3)]
    kwin = Mm.alloc([128, 512 + TG], BF16, "kwin")
    vwin = Mm.alloc([64, 16, 128], BF16, "vwin")
    bt = Mm.alloc([64, 9, 64], F32, "bt")
    sc = [Mm.alloc([64, 9, 64], F32, "sc") for _ in range(2)]
    pT = [Mm.alloc([64, 9, 64], BF16, "pT") for _ in range(2)]
    rden = [Mm.alloc([128, 64], F32, "rden") for _ in range(2)]
    ktile = Mm.alloc([128, 1024], BF16, "ktile")
    lg = Mm.alloc([128, 1024], F32, "lg")
    lg2 = Mm.alloc([128, 1024], F32, "lg2")
    gg = Mm.alloc([128, 512], F32, "gg")
    ycT = Mm.alloc([128, 16, TG], BF16, "ycT")
    gq_ = [Mm.alloc([128, 2, 128], F32, "gq") for _ in range(2)]
    gk_ = [Mm.alloc([128, 2, 128], F32, "gk") for _ in range(2)]
    gkt = [Mm.alloc([128, 256], F32, "gkt") for _ in range(2)]
    gv = [Mm.alloc([128, 512], BF16, "gv") for _ in range(2)]
    gr = [Mm.alloc([128, 512], F32, "gr") for _ in range(2)]
    eb = [Mm.alloc([128, 2, 128], F32, "eb") for _ in range(2)]
    enb = [Mm.alloc([128, 2, 128], F32, "enb") for _ in range(2)]
    erem = [Mm.alloc([128, 256], F32, "erem") for _ in range(2)]
    qd = [Mm.alloc([128, 2, 128], BF16, "qd") for _ in range(2)]
    ki = [Mm.alloc([128, 2, 128], BF16, "ki") for _ in range(2)]
    kr = [Mm.alloc([128, 256], BF16, "kr") for _ in range(2)]
    at = [Mm.alloc([128, 128], BF16, "at") for _ in range(2)]
    on_ = [Mm.alloc([128, 512], F32, "on") for _ in range(1)]
    yc = [Mm.alloc([128, 512], BF16, "yc") for _ in range(2)]

    st = {}

    def nxt(k, n):
        i = st.get(k, 0)
        st[k] = (i + 1) % n
        return i

    nck = dict(allow_slow_non_contiguous=True)

    S.dma('sp', ident_f[:], c_ident, [], ['ident_f'])
    S.dma('sp', tri[:], c_tri, [], ['tri'])
    S.dma('sp', triu[:], c_triu, [], ['triu'])
    S.dma('sp', anti[:], c_anti, [], ['anti'])
    S.op('dve', ['ident_f'], ['ident_b'], lambda: nc.vector.tensor_copy(out=ident_b[:], in_=ident_f[:]))
    S.op('dve', [], ['ones_f'], lambda: nc.vector.memset(ones_f[:], 1.0))
    S.op('dve', [], ['ones_b'], lambda: nc.vector.memset(ones_b[:], 1.0))
    S.op('dve', [], ['LA'], lambda: nc.vector.memset(LA[:], 1.0))
    S.op('dve', [], ['zero'], lambda: nc.vector.memset(zero_t[:], 0.0))

    def uh_col(sid, j, col):
        W = 2 + seqlen[sid]
        return bass.AP(UH[sid].tensor, UH[sid].offset + j * 128 * W + col, [[W, 128], [1, 1]])

    def bcast_row(row_ap, n, parts=128):
        return bass.AP(row_ap.tensor, row_ap.offset, [[0, parts], [1, n]])

    def wview(i, kcn, ncols):
        return wbuf[i][:, 0:kcn * ncols].rearrange("p (k n) -> p k n", n=ncols)

    wc = {'l': 0, 'first': True, 'off': 0, 'ch': 0}

    def load_w(w2d, k0, kcn, col0, ncols):
        i = nxt('w', NW)
        wv = wview(i, kcn, ncols)
        sz = kcn * ncols
        if wc['off'] + 128 * sz > WCH:
            wc['ch'], wc['off'] = wc['ch'] + 1, 0
        off = wc['off']
        wc['off'] = off + 128 * sz
        cache = WBF[wc['l']][wc['ch']][off:off + 128 * sz].rearrange("(p n) -> p n", p=128)
        ckey = ('WBF', wc['l'], wc['ch'], off)
        if wc['first']:
            step = 8
            for a in range(0, kcn, step):
                b = min(kcn, a + step)
                src = w2d[(k0 + a) * 128:(k0 + b) * 128, col0:col0 + ncols].rearrange("(k p) n -> p k n", p=128)
                S.dma('pool', wv[:, a:b, :], src, [], [('w', i)])
            S.dma('sp', cache, wbuf[i][:, 0:sz], [('w', i)], [ckey])
        else:
            h = sz // 2
            S.dma('pool', wbuf[i][:, 0:h], cache[:, 0:h], [ckey], [('w', i)])
            S.dma('pool', wbuf[i][:, h:sz], cache[:, h:sz], [ckey], [('w', i)])
        return i, wv

    def evac_copy(dst_sb, src_ps, reads, writes, func=None, scale=1.0, use=None):
        e = use or ('act' if nxt('evq', 2) == 0 else 'dve')
        if func is not None:
            e = 'act'
        if e == 'act':
            S.op('act', reads, writes, lambda: nc.scalar.activation(out=dst_sb, in_=src_ps, func=func or AF.Copy, scale=scale))
        elif scale != 1.0:
            S.op('dve', reads, writes, lambda: nc.vector.tensor_scalar(out=dst_sb, in0=src_ps, scalar1=scale, scalar2=None, op0=ALU.mult))
        else:
            S.op('dve', reads, writes, lambda: nc.vector.tensor_copy(out=dst_sb, in_=src_ps))

    def proj_fm(w2d, blocks, act, akey, T, evac, k0=0, KC=32):
        for (col0, ncols, ids) in blocks:
            wi, wv = load_w(w2d, k0, KC, col0, ncols)
            for sub in range((ncols + 127) // 128):
                m = min(128, ncols - sub * 128)
                b = nxt('ps', 8)

                def mm(b=b, sub=sub, m=m, wv=wv):
                    last = None
                    for kc in range(KC):
                        last = nc.tensor.matmul(ps[0:m, b, 0:T], lhsT=wv[:, kc, sub * 128:sub * 128 + m],
                                                rhs=act[:, k0 + kc, 0:T], start=(kc == 0), stop=(kc == KC - 1))
                    return last
                S.op('pe', [('w', wi), G(akey)], [('ps', b)], mm)
                evac(ids[sub], ps[0:m, b, 0:T], ('ps', b), m)

    def proj_tm(w2d, KC, blocks, act, akey, T, evac):
        nt = T // 128
        for (col0, ncols) in blocks:
            wi, wv = load_w(w2d, 0, KC, col0, ncols)
            for t in range(nt):
                b = nxt('ps', 8)

                def mm(b=b, t=t, wv=wv):
                    last = None
                    for kc in range(KC):
                        last = nc.tensor.matmul(ps[:, b, 0:ncols], lhsT=act[:, kc, t * 128:(t + 1) * 128],
                                                rhs=wv[:, kc, :], start=(kc == 0), stop=(kc == KC - 1))
                    return last
                S.op('pe', [('w', wi), G(akey)], [('ps', b)], mm)
                evac(t, col0, ncols, ps[:, b, 0:ncols], ('ps', b))

    def rstd_from(sum_ap, out_ap, scale, key_r, key_w, P=128):
        S.op('act', key_r, ['small4'], lambda: nc.scalar.activation(out=small[0:P, 4:5], in_=sum_ap, func=AF.Ln, scale=scale, bias=EPS))
        S.op('act', ['small4'], key_w, lambda: nc.scalar.activation(out=out_ap, in_=small[0:P, 4:5], func=AF.Exp, scale=-0.5))

    def rmsnorm_to_actT(xsrc, t0, T, grow):
        S.dma('sp', gtile[:], bcast_row(grow, D), [], ['gtile'])
        for t in range(T // 128):
            S.dma('sp', xt[:], xsrc[t0 + t * 128:t0 + (t + 1) * 128, :], [], ['xt'])
            S.op('dve', [], ['small0'], lambda: nc.vector.memset(small[:, 0:1], 0.0))
            S.op('act', ['xt', 'small0'], ['hb', 'small0'],
                 lambda: nc.scalar.activation(out=hb[:], in_=xt[:], func=AF.Square, accum_out=small[:, 0:1]))
            rstd_from(small[:, 0:1], small[:, 2:3], 1.0 / D, ['small0'], ['small2'])
            S.op('dve', ['xt', 'small2', 'gtile'], ['hb'],
                 lambda: nc.vector.scalar_tensor_tensor(out=hb[:], in0=xt[:], scalar=small[:, 2:3], in1=gtile[:],
                                                        op0=ALU.mult, op1=ALU.mult))
            for kq in range(8):
                b = nxt('ps', 8)
                pb = ps[:, b, :].bitcast(BF16)

                def tr(kq=kq, pb=pb):
                    last = None
                    for j in range(4):
                        kc = kq * 4 + j
                        last = nc.tensor.transpose(pb[:, j * 128:(j + 1) * 128], hb[:, kc * 128:(kc + 1) * 128], ident_b[:])
                    return last
                S.op('pe', ['hb', 'ident_b'], [('ps', b)], tr)
                dview = actT[:, kq * 4:(kq + 1) * 4, t * 128:(t + 1) * 128]
                sview = pb[:, 0:512].rearrange("p (j n) -> p j n", n=128)
                evac_copy(dview, sview, [('ps', b)], [('act', t, kq)])

    groups = [(g * TG, TG, 0) for g in range(nprompt // TG)] + [(nprompt, NSEQ * SEQ_S, 1)]
    last_pg = nprompt // TG - 1

    for l in range(depth):
        xin = x_all if l == 0 else XA
        xout = XA if l < depth - 1 else y_all
        S.barrier()
        S.dma('sp', WA2B[0:16, :], w_a2[l], [], ['WA2B'])
        S.dma('sp', WA2B[16:17, :], b_a[l:l + 1, :], [], ['WA2B'])
        S.dma('sp', gqk[:, 0:1], bass.AP(g_q.tensor, g_q[l].offset, [[1, 128], [1, 1]]), [], ['gqk'])
        S.dma('sp', gqk[:, 1:2], bass.AP(g_k.tensor, g_k[l].offset, [[1, 128], [1, 1]]), [], ['gqk'])
        for r in range(3):
            for j in range(8):
                S.dma('sp', cw[:, j, r:r + 1], bass.AP(conv_w.tensor, conv_w[l].offset + 1024 * r + 128 * j, [[1, 128], [1, 1]]), [], ['cw'])
        S.op('dve', [], ['Sst'], lambda: nc.vector.memset(Sst[:], 0.0))
        S.op('dve', [], ['Sbf'], lambda: nc.vector.memset(Sbf[:], 0.0))
        for j in range(8):
            S.dma('sp', UH[0][j, :, 0:2], zero_t[:, 0:2], ['zero'], [('UH', 0)])
        for s in range(NSEQ):
            for r in range(2):
                for j in range(8):
                    src = bass.AP(cache_conv.tensor, cache_conv[l, s].offset + 1024 * r + 128 * j, [[1, 128], [1, 1]])
                    S.dma('sp', uh_col(1 + s, j, r), src, [], [('UH', 1 + s)], **nck)
            S.dma('pool', VH[1 + s][0:512, :], cache_v[l, s], [], [('VH', 1 + s)])
            for kt in range(4):
                S.dma('pool', ktile[:], cache_k[l, s, kt * 128:(kt + 1) * 128, :], [], ['ktile'])
                for hq in range(2):
                    b = nxt('ps', 8)
                    pb = ps[:, b, :].bitcast(BF16)

                    def tr(hq=hq, pb=pb):
                        last = None
                        for j in range(4):
                            h = hq * 4 + j
                            last = nc.tensor.transpose(pb[:, j * 128:(j + 1) * 128], ktile[:, h * 128:(h + 1) * 128], ident_b[:])
                        return last
                    S.op('pe', ['ktile', 'ident_b'], [('ps', b)], tr)
                    e = nxt('mb', 3)
                    evac_copy(m_b[e][:, 0:512], pb[:, 0:512], [('ps', b)], [('mb', e)])
                    S.dma('sp', KH[1 + s][hq * 4:(hq + 1) * 4, :, kt * 128:(kt + 1) * 128].rearrange("h p n -> p h n"),
                          m_b[e][:, 0:512].rearrange("p (h n) -> p h n", n=128), [('mb', e)], [('KH', 1 + s)])
        S.op('dve', [], ['rext'], lambda: nc.vector.memset(rext_sb[:], 0.0))
        S.dma('sp', rext_sb[:, 0:257], rel_bias[l], [], ['rext'])
        S.op('dve', ['rext'], ['rext'], lambda: nc.vector.tensor_scalar(out=rext_sb[:, 257:768], in0=rext_sb[:, 257:768], scalar1=rext_sb[:, 256:257],
                                                                      scalar2=None, op0=ALU.add))
        S.dma('sp', REXT, rext_sb[:], ['rext'], ['REXT'])
        for h in range(8):
            for w in range(9):
                src = bass.AP(REXT.tensor, h * 768 + 577 - 64 * w, [[1, 64], [1, 64]])
                S.dma('sp', sc[0][:, w, :], src, ['REXT'], ['sc0'])
            for (c0, c1) in ((0, 8), (8, 9)):
                b = nxt('ps', 8)
                S.op('pe', ['sc0', 'anti'], [('ps', b)],
                     lambda b=b, c0=c0, c1=c1: nc.tensor.matmul(ps[0:64, b, 0:(c1 - c0) * 64], lhsT=anti[:],
                                                              rhs=sc[0][:, c0:c1, :].rearrange("p w q -> p (w q)"),
                                                              start=True, stop=True))
                evac_copy(sc[1][:, c0:c1, :].rearrange("p w q -> p (w q)"), ps[0:64, b, 0:(c1 - c0) * 64], [('ps', b)], ['sc1'], use='dve')
            S.dma('sp', BTD[h], sc[1][:].rearrange("p w q -> p (w q)"), ['sc1'], [('BTD', h)])

        for gi, (t0, T, is_s) in enumerate(groups):
            nt = T // 128
            segs = [(0, 0, T, t0)] if not is_s else [(1 + s, s * SEQ_S, SEQ_S, 0) for s in range(NSEQ)]
            S.barrier()
            wc['l'], wc['first'], wc['off'], wc['ch'] = l, (gi == 0), 0, 0
            rmsnorm_to_actT(xin, t0, T, g_mix[l])

            def ev_fm(idx, pap, pkey, m, T=T):
                if idx == 'LA':
                    evac_copy(LA[0:16, 0:T], pap, [pkey], ['LA'], use='dve')
                    return
                e = nxt('ev', 4)
                func = AF.Sigmoid if idx >= ZB_G else None
                scale = (256 ** -0.5) if ZB_LQ <= idx < ZB_LK else 1.0
                evac_copy(ev[e][0:m, 0:T], pap, [pkey], [('ev', e)], func=func, scale=scale)
                S.dma('sp', ZF[idx, 0:m, 0:T], ev[e][0:m, 0:T], [('ev', e)], [('ZF', idx)])

            def ev_tm(t, col0, ncols, pap, pkey, zc, c0):
                e = nxt('ev', 4)
                c = zc + (col0 - c0)
                func = AF.Silu if c >= ZT_LR else None
                evac_copy(ev[e][:, 0:ncols], pap, [pkey], [('ev', e)], func=func)
                S.dma('sp', ZT[t * 128:(t + 1) * 128, c:c + ncols], ev[e][:, 0:ncols], [('ev', e)], [('ZT', t, c // 256)])

            fm_blocks = []
            for (c0, n, zb) in ((C_CB, 1024, ZB_CB), (C_CC, 1024, ZB_CC), (C_CX, 1024, ZB_CX), (C_AQ, 1024, ZB_AQ),
                                (C_AK, 1024, ZB_AK), (C_LQ, 1024, ZB_LQ), (C_LK, 1024, ZB_LK), (C_GA, 12288, ZB_G)):
                for j in range(n // 256):
                    fm_blocks.append((c0 + j * 256, 256, [zb + 2 * j, zb + 2 * j + 1]))
            fm_blocks.append((C_LA, 16, ['LA']))
            proj_fm(w_in[l], fm_blocks, actT, 'act', T, ev_fm)
            for (c0, n, zc) in ((C_AV, 1024, ZT_AV), (C_LK, 1024, ZT_LK), (C_LV, 2048, ZT_LV), (C_LR, 2048, ZT_LR)):
                proj_tm(w_in[l], 32, [(c0 + j * 256, 256) for j in range(n // 256)], actT, 'act', T,
                        lambda t, col0, ncols, pap, pkey, zc=zc, c0=c0: ev_tm(t, col0, ncols, pap, pkey, zc, c0))

            S.barrier()
            for j in range(8):
                fa, fb, fc, fu, ft = [nxt('mf', 6) for _ in range(5)]
                S.dma('sp', m_f[fa][:, 0:T], ZF[ZB_CC + j, :, 0:T], [('ZF', ZB_CC + j)], [('mf', fa)])
                S.dma('sp', m_f[fb][:, 0:T], ZF[ZB_CX + j, :, 0:T], [('ZF', ZB_CX + j)], [('mf', fb)])
                S.dma('sp', m_f[fc][:, 0:T], ZF[ZB_CB + j, :, 0:T], [('ZF', ZB_CB + j)], [('mf', fc)])
                S.op('dve', [('mf', fa), ('mf', fb)], [('mf', fa)],
                     lambda: nc.vector.tensor_tensor(out=m_f[fa][:, 0:T], in0=m_f[fa][:, 0:T], in1=m_f[fb][:, 0:T], op=ALU.mult))
                yb = nxt('mb', 3)
                for (sid, g0, n, p0) in segs:
                    S.dma('sp', UH[sid][j, :, 2 + p0:2 + p0 + n], m_f[fa][:, g0:g0 + n], [('mf', fa)], [('UH', sid)])
                    S.dma('sp', m_f[fu][:, 0:n + 2], UH[sid][j, :, p0:p0 + n + 2], [('UH', sid)], [('mf', fu)])
                    S.op('dve', [('mf', fu), 'cw'], [('mf', ft)],
                         lambda: nc.vector.tensor_scalar(out=m_f[ft][:, 0:n], in0=m_f[fu][:, 0:n], scalar1=cw[:, j, 0:1], scalar2=None, op0=ALU.mult))
                    for r in (1, 2):
                        S.op('dve', [('mf', fu), ('mf', ft), 'cw'], [('mf', ft)],
                             lambda r=r: nc.vector.scalar_tensor_tensor(out=m_f[ft][:, 0:n], in0=m_f[fu][:, r:r + n], scalar=cw[:, j, r:r + 1],
                                                                        in1=m_f[ft][:, 0:n], op0=ALU.mult, op1=ALU.add))
                    S.op('dve', [('mf', ft), ('mf', fc)], [('mb', yb)],
                         lambda: nc.vector.tensor_tensor(out=m_b[yb][:, g0:g0 + n], in0=m_f[ft][:, 0:n], in1=m_f[fc][:, g0:g0 + n], op=ALU.mult))
                S.dma('sp', YT[j, :, 0:T], m_b[yb][:, 0:T], [('mb', yb)], [('YT', j)])
            if not is_s and gi == last_pg:
                for r in range(2):
                    for j in range(8):
                        dst = bass.AP(o_conv_p.tensor, o_conv_p[l].offset + 1024 * r + 128 * j, [[1, 128], [1, 1]])
                        S.dma('sp', dst, uh_col(0, j, nprompt + r), [('UH', 0)], [], **nck)
            if is_s:
                for s in range(NSEQ):
                    for r in range(2):
                        for j in range(8):
                            dst = bass.AP(o_conv_s.tensor, o_conv_s[l, s].offset + 1024 * r + 128 * j, [[1, 128], [1, 1]])
                            S.dma('sp', dst, uh_col(1 + s, j, SEQ_S + r), [('UH', 1 + s)], [], **nck)

            for (sid, g0, n, p0) in segs:
                S.dma('pool', VH[sid][512 + p0:512 + p0 + n, :], ZT[g0:g0 + n, ZT_AV:ZT_AV + 1024], [G('ZT')], [('VH', sid)])
                if sid == 0:
                    lo = max(p0, nprompt - NCK)
                    if lo < p0 + n:
                        S.dma('sp', o_v_p[l, lo - (nprompt - NCK):p0 + n - (nprompt - NCK), :],
                              ZT[g0 + lo - p0:g0 + n, ZT_AV:ZT_AV + 1024], [G('ZT')], [])
                else:
                    S.dma('sp', o_v_s[l, sid - 1], ZT[g0:g0 + n, ZT_AV:ZT_AV + 1024], [G('ZT')], [])
            for h in range(8):
                fq, fk, fs, fr = [nxt('mf', 6) for _ in range(4)]
                bq, bk, by = [nxt('mb', 3) for _ in range(3)]
                S.dma('sp', m_f[fq][:, 0:T], ZF[ZB_AQ + h, :, 0:T], [('ZF', ZB_AQ + h)], [('mf', fq)])
                S.dma('sp', m_f[fk][:, 0:T], ZF[ZB_AK + h, :, 0:T], [('ZF', ZB_AK + h)], [('mf', fk)])
                S.dma('sp', bt[:].rearrange("p w q -> p (w q)"), BTD[h], [('BTD', h)], ['bt'])
                for (fx, gcol, isk) in ((fq, 0, False), (fk, 1, True)):
                    S.op('act', [('mf', fx)], [('mf', fs)], lambda fx=fx: nc.scalar.activation(out=m_f[fs][:, 0:T], in_=m_f[fx][:, 0:T], func=AF.Square))
                    b = nxt('ps', 8)
                    S.op('pe', [('mf', fs), 'ones_f'], [('ps', b)],
                         lambda b=b: nc.tensor.matmul(ps[:, b, 0:T], lhsT=ones_f[:], rhs=m_f[fs][:, 0:T], start=True, stop=True))
                    S.op('act', [('ps', b)], [('mf', fr)], lambda b=b: nc.scalar.activation(out=m_f[fr][:, 0:T], in_=ps[:, b, 0:T], func=AF.Ln, scale=1.0 / 128, bias=EPS))
                    S.op('act', [('mf', fr)], [('mf', fr)], lambda: nc.scalar.activation(out=m_f[fr][:, 0:T], in_=m_f[fr][:, 0:T], func=AF.Exp, scale=-0.5))
                    if not isk:
                        S.op('dve', [('mf', fx), ('mf', fr), 'gqk'], [('mb', bq)],
                             lambda: nc.vector.scalar_tensor_tensor(out=m_b[bq][:, 0:T], in0=m_f[fq][:, 0:T], scalar=gqk[:, 0:1], in1=m_f[fr][:, 0:T],
                                                                    op0=ALU.mult, op1=ALU.mult))
                    else:
                        S.op('dve', [('mf', fx), ('mf', fr), 'gqk'], [('mf', fk)],
                             lambda: nc.vector.scalar_tensor_tensor(out=m_f[fk][:, 0:T], in0=m_f[fk][:, 0:T], scalar=gqk[:, 1:2], in1=m_f[fr][:, 0:T],
                                                                    op0=ALU.mult, op1=ALU.mult))
                        S.op('act', [('mf', fk)], [('mb', bk)], lambda: nc.scalar.activation(out=m_b[bk][:, 0:T], in_=m_f[fk][:, 0:T], func=AF.Copy))
                for (sid, g0, n, p0) in segs:
                    S.dma('sp', KH[sid][h, :, 512 + p0:512 + p0 + n], m_b[bk][:, g0:g0 + n], [('mb', bk)], [('KH', sid)])
                need_k = is_s or (t0 + T > nprompt - NCK)
                if need_k:
                    for t in range(nt):
                        b = nxt('ps', 8)
                        S.op('pe', [('mf', fk), 'ident_f'], [('ps', b)],
                             lambda b=b, t=t: nc.tensor.transpose(ps[:, b, 0:128], m_f[fk][:, t * 128:(t + 1) * 128], ident_f[:]))
                        e = nxt('xr', 4)
                        evac_copy(xr[e][:, 0:128], ps[:, b, 0:128], [('ps', b)], [('xr', e)])
                        if is_s:
                            for s in range(NSEQ):
                                S.dma('sp', o_k_s[l, s, :, h * 128:(h + 1) * 128], xr[e][s * 64:(s + 1) * 64, 0:128], [('xr', e)], [])
                        else:
                            r0 = t0 + t * 128 - (nprompt - NCK)
                            S.dma('sp', o_k_p[l, r0:r0 + 128, h * 128:(h + 1) * 128], xr[e][:, 0:128], [('xr', e)], [])
                for (sid, g0, n, p0) in segs:
                    nwc = (512 + n) // 64
                    S.dma('sp', kwin[:, 0:512 + n], KH[sid][h, :, p0:p0 + 512 + n], [('KH', sid)], ['kwin'])
                    S.dma('sp', vwin[:, 0:nwc, :], VH[sid][p0:p0 + 512 + n, h * 128:(h + 1) * 128].rearrange("(c p) d -> p c d", p=64),
                          [('VH', sid)], ['vwin'])
                    for ci in range(n // 64):
                        cpos = (p0 + 64 * ci) // 64
                        w0 = max(0, 8 - cpos) if sid == 0 else 0
                        qsl = m_b[bq][:, g0 + 64 * ci:g0 + 64 * ci + 64]
                        bA, bB = nxt('ps', 8), nxt('ps', 8)
                        i2 = nxt('att', 2)

                        def smm(bA=bA, bB=bB, w0=w0, ci=ci, qsl=qsl):
                            last = None
                            for w in range(w0, 9):
                                dst = ps[0:64, bA, w * 64:(w + 1) * 64] if w < 8 else ps[0:64, bB, 0:64]
                                last = nc.tensor.matmul(dst, lhsT=kwin[:, 64 * (ci + w):64 * (ci + w) + 64], rhs=qsl, start=True, stop=True)
                            return last
                        S.op('pe', ['kwin', ('mb', bq)], [('ps', bA), ('ps', bB)], smm)
                        if w0 < 8:
                            S.op('dve', [('ps', bA), 'bt'], [('sc', i2)],
                                 lambda: nc.vector.scalar_tensor_tensor(out=sc[i2][:, w0:8, :], in0=ps[0:64, bA, w0 * 64:512].rearrange("p (w q) -> p w q", q=64),
                                                                        scalar=128 ** -0.5, in1=bt[:, w0:8, :], op0=ALU.mult, op1=ALU.add))
                        S.op('dve', [('ps', bB), 'bt'], [('sc', i2)],
                             lambda: nc.vector.scalar_tensor_tensor(out=sc[i2][:, 8, :], in0=ps[0:64, bB, 0:64], scalar=128 ** -0.5, in1=bt[:, 8, :],
                                                                    op0=ALU.mult, op1=ALU.add))
                        S.op('act', [('sc', i2)], [('pT', i2)], lambda: nc.scalar.activation(out=pT[i2][:, w0:9, :], in_=sc[i2][:, w0:9, :], func=AF.Exp))

                        def pv(bB=bB, w0=w0, ci=ci, i2=i2):
                            last = None
                            for w in range(w0, 9):
                                last = nc.tensor.matmul(ps[:, bB, 64:128], lhsT=vwin[:, ci + w, :], rhs=pT[i2][:, w, :], start=(w == w0), stop=(w == 8))
                            for w in range(w0, 9):
                                last = nc.tensor.matmul(ps[:, bB, 128:192], lhsT=ones_b[0:64, :], rhs=pT[i2][:, w, :], start=(w == w0), stop=(w == 8))
                            return last
                        S.op('pe', ['vwin', ('pT', i2), 'ones_b'], [('ps', bB)], pv)
                        S.op('dve', [('ps', bB)], [('rden', i2)], lambda: nc.vector.reciprocal(out=rden[i2][:], in_=ps[:, bB, 128:192]))
                        S.op('dve', [('ps', bB), ('rden', i2)], [('mb', by)],
                             lambda: nc.vector.tensor_tensor(out=m_b[by][:, g0 + 64 * ci:g0 + 64 * ci + 64], in0=ps[:, bB, 64:128], in1=rden[i2][:], op=ALU.mult))
                S.dma('sp', YT[8 + h, :, 0:T], m_b[by][:, 0:T], [('mb', by)], [('YT', 8 + h)])

            S.dma('sp', gg[:], bcast_row(g_gla[l], 512), [], ['gg'])
            for (sid, g0, n, p0) in segs:
                L = 128 if sid == 0 else 64
                if sid != 0:
                    S.dma('sp', Sst[:].rearrange("p (h c) v -> p h c v", c=2),
                          state_gla[l, sid - 1].rearrange("h (c p) v -> p h c v", p=128), [], ['Sst'])
                    S.op('act', ['Sst'], ['Sbf'], lambda: nc.scalar.activation(out=Sbf[:], in_=Sst[:], func=AF.Copy))
                for bi in range(n // L):
                    gb = g0 + bi * L
                    for half in range(2):
                        b = nxt('ps', 8)
                        S.op('pe', ['LA', 'WA2B'], [('ps', b)],
                             lambda b=b, half=half: nc.tensor.matmul(ps[0:L, b, :], lhsT=LA[0:17, gb:gb + L], rhs=WA2B[0:17, half * 512:(half + 1) * 512],
                                                                     start=True, stop=True))
                        S.op('act', [('ps', b)], ['lg2'], lambda b=b, half=half: nc.scalar.activation(out=lg2[0:L, half * 512:(half + 1) * 512], in_=ps[0:L, b, :], func=AF.Exp, scale=-1.0))
                    S.op('act', ['lg2'], ['lg2'], lambda: nc.scalar.activation(out=lg2[0:L, :], in_=lg2[0:L, :], func=AF.Ln, bias=1.0))
                    S.op('dve', ['lg2'], ['lg'], lambda: nc.vector.tensor_scalar(out=lg[0:L, :], in0=lg2[0:L, :], scalar1=-1.0 / 16, scalar2=None, op0=ALU.mult))
                    for hd in range(4):
                        i2 = nxt('gla', 2)
                        for c in range(2):
                            S.dma('sp', gq_[i2][:, c, 0:L], ZF[ZB_LQ + 2 * hd + c, :, gb:gb + L], [('ZF', ZB_LQ + 2 * hd + c)], [('gq', i2)])
                            S.dma('sp', gk_[i2][:, c, 0:L], ZF[ZB_LK + 2 * hd + c, :, gb:gb + L], [('ZF', ZB_LK + 2 * hd + c)], [('gk', i2)])
                        S.dma('sp', gkt[i2][0:L, :], ZT[gb:gb + L, ZT_LK + hd * 256:ZT_LK + (hd + 1) * 256], [G('ZT')], [('gkt', i2)])
                        S.dma('pool', gv[i2][0:L, :], ZT[gb:gb + L, ZT_LV + hd * 512:ZT_LV + (hd + 1) * 512], [G('ZT')], [('gv', i2)])
                        S.dma('sp', gr[i2][0:L, :], ZT[gb:gb + L, ZT_LR + hd * 512:ZT_LR + (hd + 1) * 512], [G('ZT')], [('gr', i2)])
                        bb, brm = nxt('ps', 8), nxt('ps', 8)

                        def cums(bb=bb, brm=brm, hd=hd):
                            for c in range(2):
                                nc.tensor.matmul(ps[:, bb, c * 128:c * 128 + L], lhsT=lg[0:L, hd * 256 + c * 128:hd * 256 + (c + 1) * 128],
                                                 rhs=tri[0:L, 0:L], start=True, stop=True)
                            return nc.tensor.matmul(ps[0:L, brm, 0:256], lhsT=triu[0:L, 0:L], rhs=lg[0:L, hd * 256:(hd + 1) * 256], start=True, stop=True)
                        S.op('pe', ['lg', 'tri', 'triu'], [('ps', bb), ('ps', brm)], cums)
                        bview = ps[:, bb, 0:256].rearrange("p (c l) -> p c l", l=128)[:, :, 0:L]
                        S.op('act', [('ps', bb)], [('eb', i2)], lambda: nc.scalar.activation(out=eb[i2][:, :, 0:L], in_=bview, func=AF.Exp))
                        S.op('act', [('ps', bb)], [('enb', i2)], lambda: nc.scalar.activation(out=enb[i2][:, :, 0:L], in_=bview, func=AF.Exp, scale=-1.0))
                        S.op('act', [('ps', brm)], [('erem', i2)], lambda: nc.scalar.activation(out=erem[i2][0:L, :], in_=ps[0:L, brm, 0:256], func=AF.Exp))
                        S.op('dve', [('gq', i2), ('eb', i2)], [('qd', i2)],
                             lambda: nc.vector.tensor_tensor(out=qd[i2][:, :, 0:L], in0=gq_[i2][:, :, 0:L], in1=eb[i2][:, :, 0:L], op=ALU.mult))
                        S.op('dve', [('gk', i2), ('enb', i2)], [('ki', i2)],
                             lambda: nc.vector.tensor_tensor(out=ki[i2][:, :, 0:L], in0=gk_[i2][:, :, 0:L], in1=enb[i2][:, :, 0:L], op=ALU.mult))
                        S.op('dve', [('gkt', i2), ('erem', i2)], [('kr', i2)],
                             lambda: nc.vector.tensor_tensor(out=kr[i2][0:L, :], in0=gkt[i2][0:L, :], in1=erem[i2][0:L, :], op=ALU.mult))
                        ba = nxt('ps', 8)

                        def attm(ba=ba):
                            last = None
                            for c in range(2):
                                last = nc.tensor.matmul(ps[0:L, ba, 0:L], lhsT=ki[i2][:, c, 0:L], rhs=qd[i2][:, c, 0:L], start=(c == 0), stop=(c == 1))
                            return last
                        S.op('pe', [('ki', i2), ('qd', i2)], [('ps', ba)], attm)
                        S.op('dve', [('ps', ba), 'tri'], [('at', i2)],
                             lambda: nc.vector.tensor_tensor(out=at[i2][0:L, 0:L], in0=ps[0:L, ba, 0:L], in1=tri[0:L, 0:L], op=ALU.mult))
                        bo = nxt('ps', 8)

                        def omm(bo=bo, hd=hd):
                            for c in range(2):
                                nc.tensor.matmul(ps[0:L, bo, :], lhsT=qd[i2][:, c, 0:L], rhs=Sbf[:, hd * 2 + c, :], start=(c == 0), stop=False)
                            return nc.tensor.matmul(ps[0:L, bo, :], lhsT=at[i2][0:L, 0:L], rhs=gv[i2][0:L, :], start=False, stop=True)
                        S.op('pe', [('qd', i2), 'Sbf', ('at', i2), ('gv', i2)], [('ps', bo)], omm)
                        for c in range(2):
                            bs = nxt('ps', 8)
                            S.op('pe', [('kr', i2), ('gv', i2)], [('ps', bs)],
                                 lambda bs=bs, c=c: nc.tensor.matmul(ps[:, bs, :], lhsT=kr[i2][0:L, c * 128:(c + 1) * 128], rhs=gv[i2][0:L, :], start=True, stop=True))
                            S.op('dve', [('ps', bs), ('eb', i2), 'Sst'], ['Sst'],
                                 lambda bs=bs, c=c, hd=hd: nc.vector.scalar_tensor_tensor(out=Sst[:, hd * 2 + c, :], in0=Sst[:, hd * 2 + c, :], scalar=eb[i2][:, c, L - 1:L],
                                                                                          in1=ps[:, bs, :], op0=ALU.mult, op1=ALU.add))
                            S.op('act', ['Sst'], ['Sbf'], lambda c=c, hd=hd: nc.scalar.activation(out=Sbf[:, hd * 2 + c, :], in_=Sst[:, hd * 2 + c, :], func=AF.Copy))
                        S.op('dve', [], ['small1'], lambda: nc.vector.memset(small[:, 1:2], 0.0))
                        S.op('act', [('ps', bo), 'small1'], [('on', 0), 'small1'],
                             lambda: nc.scalar.activation(out=on_[0][0:L, :], in_=ps[0:L, bo, :], func=AF.Square, accum_out=small[0:L, 1:2]))
                        rstd_from(small[0:L, 1:2], small[0:L, 3:4], 1.0 / 512, ['small1'], ['small3'], P=L)
                        S.op('dve', [('ps', bo), 'small3', 'gg'], [('on', 0)],
                             lambda: nc.vector.scalar_tensor_tensor(out=on_[0][0:L, :], in0=ps[0:L, bo, :], scalar=small[0:L, 3:4], in1=gg[0:L, :],
                                                                    op0=ALU.mult, op1=ALU.mult))
                        S.op('dve', [('on', 0), ('gr', i2)], [('yc', i2)],
                             lambda: nc.vector.tensor_tensor(out=yc[i2][0:L, :], in0=on_[0][0:L, :], in1=gr[i2][0:L, :], op=ALU.mult))
                        bt_ = nxt('ps', 8)
                        pb = ps[:, bt_, :].bitcast(BF16)

                        def ytr(pb=pb):
                            last = None
                            for j in range(4):
                                last = nc.tensor.transpose(pb[:, j * 128:j * 128 + L], yc[i2][0:L, j * 128:(j + 1) * 128], ident_b[0:L, 0:L])
                            return last
                        S.op('pe', [('yc', i2), 'ident_b'], [('ps', bt_)], ytr)
                        evac_copy(ycT[:, hd * 4:(hd + 1) * 4, gb:gb + L], pb[:, 0:512].rearrange("p (j n) -> p j n", n=128)[:, :, 0:L],
                                  [('ps', bt_)], [('ycT', hd)])
                if sid != 0:
                    S.dma('sp', o_gla_s[l, sid - 1].rearrange("h (c p) v -> p h c v", p=128), Sst[:].rearrange("p (h c) v -> p h c v", c=2), ['Sst'], [])
                elif gi == last_pg:
                    S.dma('sp', o_gla_p[l].rearrange("h (c p) v -> p h c v", p=128), Sst[:].rearrange("p (h c) v -> p h c v", c=2), ['Sst'], [])
            S.dma('sp', YT[16:32, :, 0:T].rearrange("k p t -> p k t"), ycT[:, :, 0:T], [G('ycT')], [('YT', 16)])

            S.barrier()
            S.dma('sp', actT[:, :, 0:T], YT[:, :, 0:T].rearrange("k p t -> p k t"), [G('YT')], [('act', 0, 0)])
            for cb2 in range(16):
                wi, wv = load_w(w_branch[l], 0, 32, cb2 * 256, 256)
                for sub in range(2):
                    cb = cb2 * 2 + sub
                    for br, (k_lo, k_hi) in enumerate(((0, 8), (8, 16), (16, 32))):
                        b = nxt('ps', 8)
                        gi_ = nxt('gts', 3)
                        S.dma('sp', gts[gi_][:, 0:T], ZF[ZB_G + br * 32 + cb, :, 0:T], [('ZF', ZB_G + br * 32 + cb)], [('gts', gi_)])

                        def mm(b=b, sub=sub, wv=wv, k_lo=k_lo, k_hi=k_hi):
                            last = None
                            for kc in range(k_lo, k_hi):
                                last = nc.tensor.matmul(ps[:, b, 0:T], lhsT=wv[:, kc, sub * 128:(sub + 1) * 128], rhs=actT[:, kc, 0:T],
                                                        start=(kc == k_lo), stop=(kc == k_hi - 1))
                            return last
                        S.op('pe', [('w', wi), G('act')], [('ps', b)], mm)
                        if br == 0:
                            S.op('dve', [('ps', b), ('gts', gi_)], ['macc'],
                                 lambda b=b, gi_=gi_: nc.vector.tensor_tensor(out=macc[:, 0:T], in0=gts[gi_][:, 0:T], in1=ps[:, b, 0:T], op=ALU.mult))
                        else:
                            S.op('dve', [('ps', b), ('gts', gi_)], [('gts', gi_)],
                                 lambda b=b, gi_=gi_: nc.vector.tensor_tensor(out=gts[gi_][:, 0:T], in0=gts[gi_][:, 0:T], in1=ps[:, b, 0:T], op=ALU.mult))
                            S.op('dve', [('gts', gi_), 'macc'], ['macc'],
                                 lambda gi_=gi_: nc.vector.tensor_tensor(out=macc[:, 0:T], in0=macc[:, 0:T], in1=gts[gi_][:, 0:T], op=ALU.add))
                    S.op('act', ['macc'], [('mT', cb)], lambda cb=cb: nc.scalar.activation(out=mT[:, cb, 0:T], in_=macc[:, 0:T], func=AF.Copy))

            def ev_out(t, col0, ncols, pap, pkey, t0=t0):
                e = nxt('ev', 4)
                xi = nxt('xr', 4)
                r0 = t * 128
                S.dma('sp', xr[xi][:, 0:ncols], xin[t0 + r0:t0 + r0 + 128, col0:col0 + ncols], [], [('xr', xi)])
                S.op('dve', [pkey, ('xr', xi)], [('ev', e)],
                     lambda: nc.vector.tensor_tensor(out=ev[e][:, 0:ncols], in0=xr[xi][:, 0:ncols], in1=pap, op=ALU.add))
                S.dma('sp', XM[r0:r0 + 128, col0:col0 + ncols], ev[e][:, 0:ncols], [('ev', e)], [('XM', t)])
            proj_tm(w_out[l], 32, [(j * 256, 256) for j in range(D // 256)], mT, 'mT', T, ev_out)

            S.barrier()
            rmsnorm_to_actT(XM, 0, T, g_ffn[l])
            for half in range(2):
                gu_hold = {}

                def ev_gu(idx, pap, pkey, m, T=T, gu_hold=gu_hold):
                    j, which = idx
                    if which == 0:
                        e = nxt('ev', 4)
                        S.op('act', [pkey], [('ev', e)], lambda: nc.scalar.activation(out=ev[e][:, 0:T], in_=pap, func=AF.Silu))
                        gu_hold[j] = e
                    else:
                        e = gu_hold.pop(j)
                        S.op('dve', [pkey, ('ev', e)], [('aT', j)],
                             lambda: nc.vector.tensor_tensor(out=aT[:, j, 0:T], in0=ev[e][:, 0:T], in1=pap, op=ALU.mult))

                gu_blocks = []
                for j in range(43):
                    jj = half * 43 + j
                    gu_blocks.append((jj * 128, 128, [(j, 0)]))
                    gu_blocks.append((D_FF + jj * 128, 128, [(j, 1)]))
                proj_fm(w_gu[l], gu_blocks, actT, 'act', T, ev_gu)

                def ev_down(t, col0, ncols, pap, pkey, t0=t0, half=half):
                    e = nxt('ev', 4)
                    xi = nxt('xr', 4)
                    r0 = t * 128
                    rkey = ('xo', t, col0)
                    if half == 0:
                        S.dma('sp', xr[xi][:, 0:ncols], XM[r0:r0 + 128, col0:col0 + ncols], [('XM', t)], [('xr', xi)])
                    else:
                        S.dma('sp', xr[xi][:, 0:ncols], xout[t0 + r0:t0 + r0 + 128, col0:col0 + ncols], [rkey], [('xr', xi)])
                    S.op('dve', [pkey, ('xr', xi)], [('ev', e)],
                         lambda: nc.vector.tensor_tensor(out=ev[e][:, 0:ncols], in0=xr[xi][:, 0:ncols], in1=pap, op=ALU.add))
                    S.dma('sp', xout[t0 + r0:t0 + r0 + 128, col0:col0 + ncols], ev[e][:, 0:ncols], [('ev', e)], [rkey])

                proj_tm(w_down[l][half * 5504:(half + 1) * 5504, :], 43, [(j * 256, 256) for j in range(D // 256)], aT, 'aT', T, ev_down)

    S.finish()
    global LAST_SCHED
    LAST_SCHED = S
    return nc


def _consts():
    i = np.arange(128)
    return {
        "c_ident": np.eye(128, dtype=np.float32),
        "c_anti": np.eye(64, dtype=np.float32)[::-1].copy(),
        "c_tri": (i[:, None] <= i[None, :]).astype(np.float32),
        "c_triu": (i[:, None] > i[None, :]).astype(np.float32),
    }


def run(inputs, n_cores=8, nprompt=8192, depth=2):
    (x_prompt, x_sample, cache_conv, cache_k, cache_v, state_gla, g_mix, w_in, conv_w,
     g_q, g_k, rel_bias, w_a2, b_a, g_gla, w_branch, w_out, g_ffn, w_gu, w_down) = inputs
    f = lambda a: np.ascontiguousarray(np.asarray(a, dtype=np.float32))
    nc = build_program(nprompt, depth)
    shared = {"g_mix": f(g_mix), "w_in": f(w_in), "conv_w": f(conv_w), "g_q": f(g_q), "g_k": f(g_k),
              "rel_bias": f(rel_bias), "w_a2": f(w_a2), "b_a": f(b_a), "g_gla": f(g_gla), "w_branch": f(w_branch),
              "w_out": f(w_out), "g_ffn": f(g_ffn), "w_gu": f(w_gu), "w_down": f(w_down)}
    shared.update(_consts())
    xp = f(x_prompt)[0]
    xs = f(x_sample)
    nb = xs.shape[0]
    ck = f(cache_k).reshape(depth, nb, 512, 1024)
    cv = f(cache_v).reshape(depth, nb, 512, 1024)
    cc = f(cache_conv)
    sg = f(state_gla)
    in_maps = []
    for c in range(n_cores):
        sl = slice(c * NSEQ, (c + 1) * NSEQ)
        m = dict(shared)
        m["x_all"] = np.ascontiguousarray(np.concatenate([xp, xs[sl].reshape(NSEQ * SEQ_S, D)], axis=0))
        m["cache_conv"] = np.ascontiguousarray(cc[:, sl])
        m["cache_k"] = np.ascontiguousarray(ck[:, sl])
        m["cache_v"] = np.ascontiguousarray(cv[:, sl])
        m["state_gla"] = np.ascontiguousarray(sg[:, sl])
        in_maps.append(m)
    res = run_bass_kernel_spmd(nc, in_maps, core_ids=list(range(n_cores)))
    R = res.results
    n = n_cores
    nck = min(512, nprompt)
    y_prompt = R[0]["y_all"][:nprompt].reshape(1, nprompt, D)
    y_sample = np.concatenate([R[c]["y_all"][nprompt:].reshape(NSEQ, SEQ_S, D) for c in range(n)], axis=0)
    conv_p = R[0]["o_conv_p"].reshape(depth, 1, 2, CONV)
    k_p = R[0]["o_k_p"].reshape(depth, 1, nck, 8, 128)
    v_p = R[0]["o_v_p"].reshape(depth, 1, nck, 8, 128)
    gla_p = R[0]["o_gla_p"].reshape(depth, 1, 4, 256, 512)
    conv_s = np.concatenate([R[c]["o_conv_s"] for c in range(n)], axis=1)
    k_s = np.concatenate([R[c]["o_k_s"].reshape(depth, NSEQ, 64, 8, 128) for c in range(n)], axis=1)
    v_s = np.concatenate([R[c]["o_v_s"].reshape(depth, NSEQ, 64, 8, 128) for c in range(n)], axis=1)
    gla_s = np.concatenate([R[c]["o_gla_s"] for c in range(n)], axis=1)
    return (y_prompt, y_sample, conv_p, k_p, v_p, gla_p, conv_s, k_s, v_s, gla_s)


def kernel(x_prompt, x_sample, cache_conv, cache_k, cache_v, state_gla, g_mix, w_in, conv_w,
           g_q, g_k, rel_bias, w_a2, b_a, g_gla, w_branch, w_out, g_ffn, w_gu, w_down):
    return run((x_prompt, x_sample, cache_conv, cache_k, cache_v, state_gla, g_mix, w_in, conv_w,
                g_q, g_k, rel_bias, w_a2, b_a, g_gla, w_branch, w_out, g_ffn, w_gu, w_down))
```
